# Optimizing a Trainium2 kernel written in Bass

```python
import math
import jax, jax.numpy as jnp
from jax import lax
import numpy as np

D_MODEL = 2048
BATCH = 2
SEQ = 8192
DEPTH = 4

HEAD_DIM = 128
N_GROUP_HEADS = D_MODEL // HEAD_DIM
SGU_GROUPS = N_GROUP_HEADS // 4
RET_HEADS = (N_GROUP_HEADS - SGU_GROUPS) // 2
DIFF_HEADS = N_GROUP_HEADS - SGU_GROUPS - RET_HEADS
DIFF_MAP_DIM = HEAD_DIM // 2
RET_W = RET_HEADS * HEAD_DIM
DIFF_W = DIFF_HEADS * HEAD_DIM
SGU_W = SGU_GROUPS * HEAD_DIM
SGU_GROUP_DIM = SGU_W // SGU_GROUPS
IN_W = 4 * RET_W + 3 * DIFF_W + 2 * SGU_W
CHUNK = 128
FFN_HIDDEN = -(-8 * D_MODEL // (3 * 256)) * 256
EPS = 1e-6

kernel_name = "hybrid_retention_diffattn_sgu_block"


def rms_norm(x, g):
    xf = x.astype(jnp.float32)
    y = xf * lax.rsqrt(jnp.mean(xf * xf, axis=-1, keepdims=True) + EPS)
    return (y * g.astype(jnp.float32)).astype(x.dtype)


def retention(q, k, v, gate, gn_gain):
    B, T, _ = q.shape
    N = T // CHUNK
    H, d = RET_HEADS, HEAD_DIM
    log_gamma = jnp.log1p(-(2.0 ** (-5.0 - jnp.arange(H, dtype=jnp.float32))))

    def chunks(a):
        return a.astype(jnp.float32).reshape(B, N, CHUNK, H, d).transpose(1, 0, 3, 2, 4)

    qc = chunks(q)
    kc = chunks(k) * (d ** -0.5)
    vc = chunks(v)
    pos = jnp.arange(CHUNK, dtype=jnp.float32)
    rel = pos[:, None] - pos[None, :]
    decay = jnp.where(rel >= 0, jnp.exp(log_gamma[:, None, None] * jnp.maximum(rel, 0.0)), 0.0)
    scores = jnp.einsum('nbhcd,nbhed->nbhce', qc, kc) * decay
    inner = jnp.einsum('nbhce,nbhed->nbhcd', scores, vc)
    zeta = jnp.exp(log_gamma[:, None] * (CHUNK - 1.0 - pos))
    xi = jnp.exp(log_gamma[:, None] * (pos + 1.0))
    kv = jnp.einsum('nbhcd,nbhce->nbhde', kc * zeta[:, :, None], vc)
    chunk_decay = jnp.exp(log_gamma * CHUNK)[:, None, None]

    def step(R, kv_i):
        return kv_i + chunk_decay * R, R

    _, R_prev = lax.scan(step, jnp.zeros((B, H, d, d), jnp.float32), kv)
    cross = jnp.einsum('nbhcd,nbhde->nbhce', qc, R_prev) * xi[:, :, None]
    y = (inner + cross).transpose(1, 0, 3, 2, 4).reshape(B, T, H, d)
    mu = jnp.mean(y, axis=-1, keepdims=True)
    var = jnp.mean(jnp.square(y - mu), axis=-1, keepdims=True)
    y = (y - mu) * lax.rsqrt(var + EPS) * gn_gain.astype(jnp.float32)
    out = jax.nn.silu(gate.astype(jnp.float32)).reshape(B, T, H, d) * y
    return out.reshape(B, T, RET_W).astype(q.dtype)


def diff_attention(q, k, v, lam_q1, lam_k1, lam_q2, lam_k2, subln_gain, lambda_init):
    B, T, _ = q.shape
    H, dm, dv = DIFF_HEADS, DIFF_MAP_DIM, HEAD_DIM
    NB = T // CHUNK
    qf = q.astype(jnp.float32).reshape(B, T, H, 2, dm) * (dm ** -0.5)
    kt = k.astype(jnp.float32).reshape(B, T, H, 2, dm).transpose(0, 2, 3, 1, 4)
    vt = v.astype(jnp.float32).reshape(B, T, H, dv).transpose(0, 2, 1, 3)
    qb = qf.reshape(B, NB, CHUNK, H, 2, dm).transpose(1, 0, 3, 4, 2, 5)
    lam = (jnp.exp(jnp.sum(lam_q1.astype(jnp.float32) * lam_k1.astype(jnp.float32)))
           - jnp.exp(jnp.sum(lam_q2.astype(jnp.float32) * lam_k2.astype(jnp.float32)))
           + lambda_init)
    slopes = 2.0 ** (-8.0 * jnp.arange(1, H + 1, dtype=jnp.float32) / H)
    key_pos = jnp.arange(T)

    def block(args):
        q_blk, start = args
        qpos = start + jnp.arange(CHUNK)
        dist = (qpos[:, None] - key_pos[None, :]).astype(jnp.float32)
        s = jnp.einsum('bhmcd,bhmtd->bhmct', q_blk, kt) - slopes[None, :, None, None, None] * dist
        s = jnp.where(dist >= 0, s, -jnp.inf)
        p = jax.nn.softmax(s, axis=-1)
        a = p[:, :, 0] - lam * p[:, :, 1]
        return jnp.einsum('bhct,bhte->bhce', a, vt)

    starts = jnp.arange(NB) * CHUNK
    o = lax.map(block, (qb, starts))
    o = o.transpose(1, 0, 3, 2, 4).reshape(B, T, H, dv)
    o = o * lax.rsqrt(jnp.mean(o * o, axis=-1, keepdims=True) + EPS) * subln_gain.astype(jnp.float32)
    o = o * (1.0 - lambda_init)
    return o.reshape(B, T, DIFF_W).astype(q.dtype)


def spatial_gating(u, v, ln_g, ln_b, w_s, b_s):
    B, T, _ = u.shape
    N = T // CHUNK
    G, dg = SGU_GROUPS, SGU_GROUP_DIM
    uf = jax.nn.gelu(u.astype(jnp.float32)).reshape(B, N, CHUNK, G, dg)
    vf = jax.nn.gelu(v.astype(jnp.float32)).reshape(B, N, CHUNK, G, dg)
    mu = jnp.mean(vf, axis=-1, keepdims=True)
    var = jnp.mean(jnp.square(vf - mu), axis=-1, keepdims=True)
    vf = (vf - mu) * lax.rsqrt(var + EPS) * ln_g.astype(jnp.float32) + ln_b.astype(jnp.float32)
    mask = jnp.tril(jnp.ones((CHUNK, CHUNK), jnp.float32))
    w = w_s.astype(jnp.float32) * mask
    mixed = jnp.einsum('gts,bnsgd->bntgd', w, vf) + b_s.astype(jnp.float32).T[None, None, :, :, None]
    return (uf * mixed).reshape(B, T, SGU_W).astype(u.dtype)


def setup_inputs(seed: int = 0) -> dict:
    key = jax.random.key(seed)
    ks = jax.random.split(key, 24)
    f32 = jnp.float32
    L, D = DEPTH, D_MODEL

    def nrm(k, shape, scale):
        return jax.random.normal(k, shape, f32) * scale

    def gain(k, shape):
        return 1.0 + 0.05 * jax.random.normal(k, shape, f32)

    return {
        "x": jax.random.normal(ks[0], (BATCH, SEQ, D), f32),
        "pre_mix_g": gain(ks[1], (L, D)),
        "w_in": nrm(ks[2], (L, D, IN_W), D ** -0.5),
        "ret_gn_g": gain(ks[3], (L, RET_HEADS, HEAD_DIM)),
        "diff_lam_q1": nrm(ks[4], (L, DIFF_MAP_DIM), 0.1),
        "diff_lam_k1": nrm(ks[5], (L, DIFF_MAP_DIM), 0.1),
        "diff_lam_q2": nrm(ks[6], (L, DIFF_MAP_DIM), 0.1),
        "diff_lam_k2": nrm(ks[7], (L, DIFF_MAP_DIM), 0.1),
        "diff_subln_g": gain(ks[8], (L, DIFF_HEADS, HEAD_DIM)),
        "sgu_ln_g": gain(ks[9], (L, SGU_GROUPS, SGU_GROUP_DIM)),
        "sgu_ln_b": nrm(ks[10], (L, SGU_GROUPS, SGU_GROUP_DIM), 0.02),
        "sgu_w": nrm(ks[11], (L, SGU_GROUPS, CHUNK, CHUNK), CHUNK ** -0.5),
        "sgu_b": 1.0 + nrm(ks[12], (L, SGU_GROUPS, CHUNK), 0.1),
        "w_out": nrm(ks[13], (L, D, D), D ** -0.5),
        "post_mix_g": gain(ks[14], (L, D)),
        "pre_ffn_g": gain(ks[15], (L, D)),
        "w_gate": nrm(ks[16], (L, D, FFN_HIDDEN), D ** -0.5),
        "w_up": nrm(ks[17], (L, D, FFN_HIDDEN), D ** -0.5),
        "w_down": nrm(ks[18], (L, FFN_HIDDEN, D), FFN_HIDDEN ** -0.5),
        "post_ffn_g": gain(ks[19], (L, D)),
    }


def reference(x, pre_mix_g, w_in, ret_gn_g, diff_lam_q1, diff_lam_k1, diff_lam_q2, diff_lam_k2,
              diff_subln_g, sgu_ln_g, sgu_ln_b, sgu_w, sgu_b, w_out, post_mix_g, pre_ffn_g,
              w_gate, w_up, w_down, post_ffn_g):
    split_points = [RET_W, 2 * RET_W, 3 * RET_W, 4 * RET_W,
                    4 * RET_W + DIFF_W, 4 * RET_W + 2 * DIFF_W, 4 * RET_W + 3 * DIFF_W,
                    4 * RET_W + 3 * DIFF_W + SGU_W]
    for l in range(DEPTH):
        lambda_init = 0.8 - 0.6 * math.exp(-0.3 * l)
        h = rms_norm(x, pre_mix_g[l])
        proj = h @ w_in[l]
        rq, rk, rv, rg, dq, dk, dv, su, sv = jnp.split(proj, split_points, axis=-1)
        y_ret = retention(rq, rk, rv, rg, ret_gn_g[l])
        y_diff = diff_attention(dq, dk, dv, diff_lam_q1[l], diff_lam_k1[l], diff_lam_q2[l],
                                diff_lam_k2[l], diff_subln_g[l], lambda_init)
        y_sgu = spatial_gating(su, sv, sgu_ln_g[l], sgu_ln_b[l], sgu_w[l], sgu_b[l])
        mix = jnp.concatenate([y_ret, y_diff, y_sgu], axis=-1) @ w_out[l]
        x = x + rms_norm(mix, post_mix_g[l])
        h = rms_norm(x, pre_ffn_g[l])
        f = (jax.nn.silu(h @ w_gate[l]) * (h @ w_up[l])) @ w_down[l]
        x = x + rms_norm(f, post_ffn_g[l])
    return x
```

```python
import math
import numpy as np
import ml_dtypes
import concourse.bass as bass
import concourse.mybir as mybir
from concourse.bass_utils import run_bass_kernel_spmd

F32 = mybir.dt.float32
BF16 = mybir.dt.bfloat16
AF = mybir.ActivationFunctionType
ALU = mybir.AluOpType
AX = mybir.AxisListType

D = 2048
TL = 2048
NKC = 16
HID = 5632
NHC = 44
DEPTH = 4
EPS = 1e-6
NEG = -30000.0

_TAB = {}
_o = 0
for _n, _w in (("gB", 48), ("retg", 6), ("subg", 6), ("lng", 512), ("lnb", 512), ("sgub", 512), ("sguw", 512),
               ("lamv", 256), ("lconst", 2), ("bias", 1536), ("decT", 768), ("zeta", 6), ("xiT", 768),
               ("coef", 24), ("tril", 128)):
    _TAB[_n] = (_o, _w)
    _o += _w
NTAB = _o
NTABB = 896 + 512 + 128


class Buf:
    def __init__(self, name, ro=False):
        self.name = name
        self.last_w = None
        self.readers = []
        self.slot = None
        self.ro = ro


class Sched:
    def __init__(self, nc):
        self.nc = nc
        self.eng = {}
        for nm, obj in (("pe", nc.tensor), ("act", nc.scalar), ("dve", nc.vector),
                        ("pool", nc.gpsimd), ("sp", nc.sync)):
            self.eng[nm] = dict(obj=obj, sem=nc.alloc_semaphore("s_" + nm), count=0, waited={})
        self.pool_slots = {"sp": [], "pool": []}
        self.all_slots = []
        self.phase_bufs = []

    def buf(self, name, persist=False, ro=False):
        b = Buf(name, ro)
        if not persist:
            self.phase_bufs.append(b)
        return b

    def _wait(self, e, toks):
        E = self.eng[e]
        best = {}
        for t in toks:
            if t is None:
                continue
            sem, val = t
            k = id(sem)
            if k not in best or best[k][1] < val:
                best[k] = (sem, val)
        for k, (sem, val) in best.items():
            if e == "pe" and sem is E["sem"]:
                continue
            if E["waited"].get(k, 0) < val:
                E["obj"].wait_ge(sem, val)
                E["waited"][k] = val

    def op(self, e, fn, reads=(), writes=(), inc=True):
        E = self.eng[e]
        toks = []
        for r in reads:
            toks.append(r.last_w)
        for w in writes:
            toks.append(w.last_w)
            toks.extend(w.readers)
        self._wait(e, toks)
        ins = fn(E["obj"])
        if inc:
            ins.then_inc(E["sem"], 1)
            E["count"] += 1
            tok = (E["sem"], E["count"])
        else:
            tok = (E["sem"], E["count"] + 1)
        for w in writes:
            w.last_w = tok
            w.readers = []
        for r in reads:
            if not r.ro:
                r.readers.append(tok)
        return ins

    def dma(self, q, out_ap, in_ap, dst, src, **kw):
        E = self.eng[q]
        toks = [src.last_w, dst.last_w] + list(dst.readers)
        self._wait(q, toks)
        if dst.slot is None:
            if self.pool_slots[q]:
                dst.slot = self.pool_slots[q].pop()
            else:
                dst.slot = [self.nc.alloc_semaphore("d%s%d" % (q, len(self.all_slots))), 0, q]
                self.all_slots.append(dst.slot)
        assert dst.slot[2] == q, (dst.name, q)
        ins = E["obj"].dma_start(out=out_ap, in_=in_ap, **kw)
        ins.then_inc(dst.slot[0], 16)
        dst.slot[1] += 16
        tok = (dst.slot[0], dst.slot[1])
        dst.last_w = tok
        dst.readers = []
        if not src.ro:
            src.readers.append(tok)
        return ins

    def barrier(self):
        toks = [(E["sem"], E["count"]) for E in self.eng.values() if E["count"] > 0]
        toks += [(s[0], s[1]) for s in self.all_slots if s[1] > 0]
        for e in self.eng:
            self._wait(e, toks)

    def end_phase(self):
        self.barrier()
        for b in self.phase_bufs:
            if b.slot is not None:
                self.pool_slots[b.slot[2]].append(b.slot)
                b.slot = None
        self.phase_bufs = []


def _act(S, out, in_, func, reads, writes, bias=0.0, scale=1.0):
    S.op("act", lambda e: e.activation(out=out, in_=in_, func=func, bias=bias, scale=scale), reads=reads, writes=writes)


class Ctx:
    pass


_cnt = [0]


def SB(nc, name, shape, dt):
    _cnt[0] += 1
    return nc.sbuf_tensor("%s_%d" % (name, _cnt[0]), shape, dt)


def _setup(nc):
    C = Ctx()
    C.nc = nc
    C.S = Sched(nc)
    C.PS = []
    C.PB = []
    for i in range(8):
        C.PS.append(nc.alloc_psum_tensor("ps%d" % i, [128, 512], F32).ap())
        C.PB.append(C.S.buf("ps%d" % i, persist=True))
    C.big = nc.alloc_sbuf_tensor("big", [128, 16, TL], BF16).ap()
    C.ones = nc.alloc_sbuf_tensor("ones", [128, 128], BF16).ap()
    C.Bones = C.S.buf("ones", persist=True, ro=True)
    C.S.op("dve", lambda e: e.memset(C.ones, 1.0), writes=[C.Bones])
    C.bank_rr = 0
    return C


def _rms_group(C, src, Bsrc, rstd, Brstd, sq, Bsq, bank, nchunks, inv_n):
    S = C.S
    ps, pb = C.PS[bank], C.PB[bank]
    for c in range(nchunks):
        k = c % 2
        _act(S, sq[k], src[:, c, :], AF.Square, [Bsrc], [Bsq[k]])
        S.op("pe", lambda e: e.matmul(ps, C.ones, sq[k], start=(c == 0), stop=(c == nchunks - 1)),
             reads=[C.Bones, Bsq[k]], writes=[pb], inc=True)
    _act(S, rstd, ps, AF.Sqrt, [pb], [Brstd], bias=EPS, scale=inv_n)
    S.op("dve", lambda e: e.reciprocal(out=rstd, in_=rstd), reads=[Brstd], writes=[Brstd])


def stage_A(C, io):
    nc, S = C.nc, C.S
    xT, win_fm, win_tm, tabA = io["xT"], io["win_fm"], io["win_tm"], io["tabA"]
    pfm, ptm, sloc = io["pfm"], io["ptm"], io["sloc"]
    Bx, Bw, Btab = io["BxT"], S.buf("w", ro=True), S.buf("tabA_d", ro=True)
    Bpfm, Bptm, Bsloc = io["Bpfm"], io["Bptm"], io["Bsloc"]
    hT = C.big
    Bh = [S.buf("hT%d" % g) for g in range(4)]
    with SB(nc, "tabA", [128, 16 + 96], F32) as tA_, \
            SB(nc, "xg", [128, 16, 512], F32) as xg_, \
            SB(nc, "sq", [128, 2, 512], BF16) as sq_, \
            SB(nc, "rstd", [128, 512], F32) as rstd_, \
            SB(nc, "wfm", [128, 3, 16, 128], BF16) as wfm_, \
            SB(nc, "wtm", [128, 2, 16, 512], BF16) as wtm_, \
            SB(nc, "stf", [128, 2, TL], BF16) as stf_, \
            SB(nc, "stt", [128, 4, 512], BF16) as stt_, \
            SB(nc, "ktv", [128, 2, 1536], BF16) as ktv_, \
            SB(nc, "kz", [128, 2, 128], BF16) as kz_, \
            SB(nc, "ssb", [128, 768], F32) as ssb_:
        tA, xg, sq, rstd, wfm, wtm, stf, stt, ktv, kz, ssb = (t.ap() for t in (tA_, xg_, sq_, rstd_, wfm_, wtm_, stf_, stt_, ktv_, kz_, ssb_))
        BtA, Bxg, Brstd = S.buf("tA"), S.buf("xg"), S.buf("rstd")
        Bsq = [S.buf("sq0"), S.buf("sq1")]
        S.dma("sp", tA, tabA, BtA, Btab)
        for g in range(4):
            S.dma("sp", xg, xT[:, :, g * 512:(g + 1) * 512].rearrange("c p t -> p c t"), Bxg, Bx)
            _rms_group(C, xg, Bxg, rstd, Brstd, [sq[:, 0, :], sq[:, 1, :]], Bsq, 0, 16, 1.0 / D)
            for c in range(16):
                S.op("dve", lambda e: e.scalar_tensor_tensor(out=hT[:, c, g * 512:(g + 1) * 512], in0=xg[:, c, :],
                                                             scalar=tA[:, c:c + 1], in1=rstd, op0=ALU.mult, op1=ALU.mult),
                     reads=[Bxg, BtA, Brstd], writes=[Bh[g]])
        Bwf = [S.buf("wfm%d" % i) for i in range(3)]
        Bstf = [S.buf("stf%d" % i) for i in range(2)]
        ev = 0
        for blk in range(34):
            sl = blk % 3
            S.dma("pool", wfm[:, sl], win_fm[blk], Bwf[sl], Bw)
            st = blk % 2
            for tg in range(4):
                bk = 1 + (C.bank_rr % 7)
                C.bank_rr += 1
                ps, pb = C.PS[bk], C.PB[bk]
                for kc in range(16):
                    S.op("pe", lambda e: e.matmul(ps, wfm[:, sl, kc, :], hT[:, kc, tg * 512:(tg + 1) * 512],
                                                  start=(kc == 0), stop=(kc == 15)),
                         reads=[Bwf[sl], Bh[tg]], writes=[pb], inc=(kc == 15))
                dst = stf[:, st, tg * 512:(tg + 1) * 512]
                if ev % 2 == 0:
                    S.op("act", lambda e: e.copy(out=dst, in_=ps), reads=[pb], writes=[Bstf[st]])
                else:
                    S.op("dve", lambda e: e.tensor_copy(out=dst, in_=ps), reads=[pb], writes=[Bstf[st]])
                ev += 1
            S.dma("sp", pfm[blk], stf[:, st, :], Bpfm, Bstf[st])
        Bwt = [S.buf("wtm%d" % i) for i in range(2)]
        Bstt = [S.buf("stt%d" % i) for i in range(4)]
        n = 0
        for cc in range(6):
            ncol = 512 if cc < 5 else 256
            sl = cc % 2
            S.dma("pool", wtm[:, sl, :, 0:ncol], win_tm[:, :, cc * 512:cc * 512 + ncol], Bwt[sl], Bw)
            for tt in range(16):
                bk = 1 + (C.bank_rr % 7)
                C.bank_rr += 1
                ps, pb = C.PS[bk], C.PB[bk]
                for kc in range(16):
                    S.op("pe", lambda e: e.matmul(ps[:, 0:ncol], hT[:, kc, tt * 128:(tt + 1) * 128], wtm[:, sl, kc, 0:ncol],
                                                  start=(kc == 0), stop=(kc == 15)),
                         reads=[Bwt[sl], Bh[tt // 4]], writes=[pb], inc=(kc == 15))
                st = n % 4
                dst = stt[:, st, 0:ncol]
                if n % 2 == 0:
                    S.op("act", lambda e: e.copy(out=dst, in_=ps[:, 0:ncol]), reads=[pb], writes=[Bstt[st]])
                else:
                    S.op("dve", lambda e: e.tensor_copy(out=dst, in_=ps[:, 0:ncol]), reads=[pb], writes=[Bstt[st]])
                n += 1
                S.dma("sp", ptm[tt, :, cc * 512:cc * 512 + ncol], dst, Bptm, Bstt[st])
        Bktv = [S.buf("ktv0"), S.buf("ktv1")]
        Bkz = [S.buf("kz0"), S.buf("kz1")]
        Bssb = S.buf("ssb")
        accb = [C.PB[1], C.PB[2]]
        n = 0
        for i in range(16):
            sl = i % 2
            S.dma("sp", ktv[:, sl, :], ptm[i, :, 0:1536], Bktv[sl], Bptm)
            for h in range(6):
                k2 = n % 2
                n += 1
                S.op("dve", lambda e: e.tensor_scalar(out=kz[:, k2, :], in0=ktv[:, sl, h * 128:(h + 1) * 128],
                                                      scalar1=tA[:, 16 + i * 6 + h:16 + i * 6 + h + 1], scalar2=None, op0=ALU.mult),
                     reads=[Bktv[sl], BtA], writes=[Bkz[k2]])
                bk = 1 + h // 4
                S.op("pe", lambda e: e.matmul(C.PS[bk][:, (h % 4) * 128:(h % 4 + 1) * 128], kz[:, k2, :],
                                              ktv[:, sl, 768 + h * 128:768 + (h + 1) * 128], start=(i == 0), stop=(i == 15)),
                     reads=[Bkz[k2], Bktv[sl]], writes=[accb[h // 4]], inc=True)
        S.op("dve", lambda e: e.tensor_copy(out=ssb[:, 0:512], in_=C.PS[1]), reads=[accb[0]], writes=[Bssb])
        S.op("dve", lambda e: e.tensor_copy(out=ssb[:, 512:768], in_=C.PS[2][:, 0:256]), reads=[accb[1]], writes=[Bssb])
        S.dma("sp", sloc, ssb, Bsloc, Bssb)
        S.end_phase()


def stage_B(C, io):
    nc, S = C.nc, C.S
    PS, PB = C.PS, C.PB
    xT, xo, pfm, ptm = io["xT"], io["xo"], io["pfm"], io["ptm"]
    kall, vall, sall = io["kall"], io["vall"], io["sall"]
    wout, wg, wu, wd = io["wout"], io["wg"], io["wu"], io["wd"]
    tabs_d, tabb_d, qaug_d = io["tabs"], io["tabb"], io["qaug"]
    mixT, fT = io["mixT"], io["fT"]
    Bx, Bxo, Bpfm, Bptm = io["BxT"], io["Bxo"], io["Bpfm"], io["Bptm"]
    Bro = S.buf("ro", ro=True)
    BmixT, BfT = S.buf("mixT", persist=True), S.buf("fT", persist=True)
    ycat = C.big
    By = [[S.buf("y%d_%d" % (c, g), persist=True) for g in range(4)] for c in range(16)]

    def T(name, lo=0, hi=None):
        o, w = _TAB[name]
        return (o + lo, o + (w if hi is None else hi))

    with SB(nc, "tabs", [128, NTAB], F32) as tabs_, SB(nc, "tabb", [128, NTABB], BF16) as tabb_, \
            SB(nc, "lam", [128, 8], F32) as lam_:
        tabs, tabb, lam = tabs_.ap(), tabb_.ap(), lam_.ap()
        Btabs, Btabb, Blam = S.buf("tabs", persist=True), S.buf("tabb", persist=True), S.buf("lam", persist=True)
        S.dma("sp", tabs, tabs_d, Btabs, Bro)
        S.dma("sp", tabb, tabb_d, Btabb, Bro)
        wmask = tabb[:, 0:896]
        dj = tabb[:, 896:896 + 512]
        cmat = tabb[:, 1408:1536]

        def tcol(name, i):
            o, _ = _TAB[name]
            return tabs[:, o + i:o + i + 1]

        def tsl(name, lo, hi):
            o, _ = _TAB[name]
            return tabs[:, o + lo:o + hi]

        with SB(nc, "ltmp", [128, 2, 64], F32) as ltmp_:
            ltmp = ltmp_.ap()
            Bl = S.buf("ltmp")
            S.op("dve", lambda e: e.tensor_tensor(out=ltmp[:, 0, :], in0=tsl("lamv", 0, 64), in1=tsl("lamv", 64, 128), op=ALU.mult),
                 reads=[Btabs], writes=[Bl])
            S.op("dve", lambda e: e.tensor_tensor(out=ltmp[:, 1, :], in0=tsl("lamv", 128, 192), in1=tsl("lamv", 192, 256), op=ALU.mult),
                 reads=[Btabs], writes=[Bl])
            S.op("dve", lambda e: e.reduce_sum(out=lam[:, 0:1], in_=ltmp[:, 0, :], axis=AX.X), reads=[Bl], writes=[Blam])
            S.op("dve", lambda e: e.reduce_sum(out=lam[:, 1:2], in_=ltmp[:, 1, :], axis=AX.X), reads=[Bl], writes=[Blam])
            _act(S, lam[:, 2:4], lam[:, 0:2], AF.Exp, [Blam], [Blam])
            S.op("dve", lambda e: e.tensor_tensor(out=lam[:, 4:5], in0=lam[:, 3:4], in1=lam[:, 2:3], op=ALU.subtract), reads=[Blam], writes=[Blam])
            S.op("dve", lambda e: e.tensor_tensor(out=lam[:, 4:5], in0=lam[:, 4:5], in1=tcol("lconst", 0), op=ALU.subtract), reads=[Blam, Btabs], writes=[Blam])
            S.end_phase()

        with SB(nc, "wTb", [128, 4, 128], BF16) as wTb_, SB(nc, "svt", [128, 2, 512], BF16) as svt_, \
                SB(nc, "vf", [128, 512], F32) as vf_, SB(nc, "vsq", [128, 512], F32) as vsq_, \
                SB(nc, "st4", [128, 16], F32) as st4_, SB(nc, "vn", [128, 2, 512], BF16) as vn_, \
                SB(nc, "sut", [128, 2, 4, 128], BF16) as sut_, SB(nc, "gu", [128, 512], F32) as gu_, \
                SB(nc, "mx", [128, 512], F32) as mx_:
            wTb, svt, vf, vsq, st4, vn, sut, gu, mx = (t.ap() for t in (wTb_, svt_, vf_, vsq_, st4_, vn_, sut_, gu_, mx_))
            BwTb, Bvf, Bvsq, Bst4, Bgu, Bmx = (S.buf(n) for n in ("wTb", "vf", "vsq", "st4", "gu", "mx"))
            Bsvt = [S.buf("svt0"), S.buf("svt1")]
            Bvn = [S.buf("vn0"), S.buf("vn1")]
            Bsut = [S.buf("sut0"), S.buf("sut1")]
            for grp in range(4):
                S.op("dve", lambda e: e.tensor_tensor(out=wTb[:, grp, :], in0=tsl("sguw", grp * 128, (grp + 1) * 128),
                                                      in1=tsl("tril", 0, 128), op=ALU.mult), reads=[Btabs], writes=[BwTb])
            for i in range(16):
                sl = i % 2
                S.dma("sp", svt[:, sl, :], ptm[i, :, 2304:2816], Bsvt[sl], Bptm)
                S.dma("sp", sut[:, sl], pfm[30:34, :, i * 128:(i + 1) * 128].rearrange("b p t -> p b t"), Bsut[sl], Bpfm)
                _act(S, vf, svt[:, sl, :], AF.Gelu_apprx_tanh, [Bsvt[sl]], [Bvf])
                vf3 = vf.rearrange("p (g d) -> p g d", g=4)
                S.op("dve", lambda e: e.reduce_sum(out=st4[:, 0:4], in_=vf3, axis=AX.X), reads=[Bvf], writes=[Bst4])
                S.op("dve", lambda e: e.tensor_tensor(out=vsq, in0=vf, in1=vf, op=ALU.mult), reads=[Bvf], writes=[Bvsq])
                S.op("dve", lambda e: e.reduce_sum(out=st4[:, 4:8], in_=vsq.rearrange("p (g d) -> p g d", g=4), axis=AX.X), reads=[Bvsq], writes=[Bst4])
                S.op("dve", lambda e: e.tensor_scalar(out=st4[:, 0:4], in0=st4[:, 0:4], scalar1=1.0 / 128, scalar2=None, op0=ALU.mult), reads=[Bst4], writes=[Bst4])
                S.op("dve", lambda e: e.tensor_tensor(out=st4[:, 8:12], in0=st4[:, 0:4], in1=st4[:, 0:4], op=ALU.mult), reads=[Bst4], writes=[Bst4])
                S.op("dve", lambda e: e.scalar_tensor_tensor(out=st4[:, 8:12], in0=st4[:, 4:8], scalar=1.0 / 128, in1=st4[:, 8:12],
                                                             op0=ALU.mult, op1=ALU.subtract), reads=[Bst4], writes=[Bst4])
                _act(S, st4[:, 12:16], st4[:, 8:12], AF.Sqrt, [Bst4], [Bst4], bias=EPS)
                S.op("dve", lambda e: e.reciprocal(out=st4[:, 12:16], in_=st4[:, 12:16]), reads=[Bst4], writes=[Bst4])
                for grp in range(4):
                    S.op("dve", lambda e: e.tensor_scalar(out=vf[:, grp * 128:(grp + 1) * 128], in0=vf[:, grp * 128:(grp + 1) * 128],
                                                          scalar1=st4[:, grp:grp + 1], scalar2=st4[:, 12 + grp:13 + grp],
                                                          op0=ALU.subtract, op1=ALU.mult), reads=[Bvf, Bst4], writes=[Bvf])
                S.op("dve", lambda e: e.tensor_tensor(out=vf, in0=vf, in1=tsl("lng", 0, 512), op=ALU.mult), reads=[Bvf, Btabs], writes=[Bvf])
                S.op("dve", lambda e: e.tensor_tensor(out=vn[:, sl, :], in0=vf, in1=tsl("lnb", 0, 512), op=ALU.add), reads=[Bvf, Btabs], writes=[Bvn[sl]])
                bk = 1 + (i % 2)
                for grp in range(4):
                    S.op("pe", lambda e: e.matmul(PS[bk][:, grp * 128:(grp + 1) * 128], vn[:, sl, grp * 128:(grp + 1) * 128], wTb[:, grp, :],
                                                  start=True, stop=True), reads=[Bvn[sl], BwTb], writes=[PB[bk]], inc=(grp == 3))
                _act(S, gu, sut[:, sl].rearrange("p b t -> p (b t)"), AF.Gelu_apprx_tanh, [Bsut[sl]], [Bgu])
                S.op("dve", lambda e: e.tensor_tensor(out=mx, in0=PS[bk], in1=tsl("sgub", 0, 512), op=ALU.add), reads=[PB[bk], Btabs], writes=[Bmx])
                for grp in range(4):
                    S.op("dve", lambda e: e.tensor_tensor(out=ycat[:, 12 + grp, i * 128:(i + 1) * 128], in0=mx[:, grp * 128:(grp + 1) * 128],
                                                          in1=gu[:, grp * 128:(grp + 1) * 128], op=ALU.mult),
                         reads=[Bmx, Bgu], writes=[By[12 + grp][i // 4]])
            S.end_phase()

        GC = [math.exp(128.0 * math.log1p(-(2.0 ** (-5.0 - h)))) for h in range(6)]
        with SB(nc, "sg", [128, 4, 768], F32) as sg_, SB(nc, "R", [128, 6, 128], F32) as R_, \
                SB(nc, "Rb", [128, 6, 128], BF16) as Rb_, SB(nc, "qkg", [128, 2, 18, 512], BF16) as qkg_, \
                SB(nc, "ktv2", [128, 2, 1536], BF16) as ktv_, SB(nc, "sTd", [128, 2, 128], BF16) as sTd_, \
                SB(nc, "qx", [128, 2, 128], BF16) as qx_, SB(nc, "kz2", [128, 2, 128], BF16) as kz_, \
                SB(nc, "yb", [128, 6, 512], BF16) as yb_, SB(nc, "ysq", [128, 2, 512], BF16) as ysq_, \
                SB(nc, "rs2", [128, 512], F32) as rs2_, SB(nc, "sgt", [128, 512], F32) as sgt_, \
                SB(nc, "t1", [128, 512], F32) as t1_:
            sg, R, Rb, qkg, ktv, sTd, qx, kz, yb, ysq, rs2, sgt, t1 = (t.ap() for t in (sg_, R_, Rb_, qkg_, ktv_, sTd_, qx_, kz_, yb_, ysq_, rs2_, sgt_, t1_))
            Bsg = S.buf("sg")
            BR = [S.buf("R%d" % h) for h in range(6)]
            BRb = [S.buf("Rb%d" % h) for h in range(6)]
            Bqkg = [S.buf("qkg0"), S.buf("qkg1")]
            Bktv = [S.buf("ktv0"), S.buf("ktv1")]
            BsTd = [S.buf("sTd0"), S.buf("sTd1")]
            Bqx = [S.buf("qx0"), S.buf("qx1")]
            Bkz = [S.buf("kz0"), S.buf("kz1")]
            Byb = [S.buf("yb%d" % h) for h in range(6)]
            Bysq = [S.buf("ysq0"), S.buf("ysq1")]
            Brs2, Bsgt, Bt1 = S.buf("rs2"), S.buf("sgt"), S.buf("t1")
            S.dma("sp", sg, sall.rearrange("j p f -> p j f"), Bsg, Bro)
            for h in range(6):
                S.op("dve", lambda e: e.tensor_scalar(out=R[:, h, :], in0=sg[:, 0, h * 128:(h + 1) * 128], scalar1=tcol("coef", 0 * 6 + h),
                                                      scalar2=None, op0=ALU.mult), reads=[Bsg, Btabs], writes=[BR[h]])
                for j in range(1, 4):
                    S.op("dve", lambda e: e.scalar_tensor_tensor(out=R[:, h, :], in0=sg[:, j, h * 128:(h + 1) * 128], scalar=tcol("coef", j * 6 + h),
                                                                 in1=R[:, h, :], op0=ALU.mult, op1=ALU.add), reads=[Bsg, Btabs, BR[h]], writes=[BR[h]])
                S.op("dve", lambda e: e.tensor_copy(out=Rb[:, h, :], in_=R[:, h, :]), reads=[BR[h]], writes=[BRb[h]])
            n = 0
            for g in range(4):
                gs = g % 2
                S.dma("sp", qkg[:, gs], pfm[0:18, :, g * 512:(g + 1) * 512].rearrange("b p t -> p b t"), Bqkg[gs], Bpfm)
                for ci in range(4):
                    i = g * 4 + ci
                    sl = i % 2
                    S.dma("sp", ktv[:, sl, :], ptm[i, :, 0:1536], Bktv[sl], Bptm)
                    for h in range(6):
                        k2 = n % 2
                        n += 1
                        qT = qkg[:, gs, h, ci * 128:(ci + 1) * 128]
                        kT = qkg[:, gs, 6 + h, ci * 128:(ci + 1) * 128]
                        ktok = ktv[:, sl, h * 128:(h + 1) * 128]
                        vtok = ktv[:, sl, 768 + h * 128:768 + (h + 1) * 128]
                        b_s, b_y, b_kv = 1 + (n % 2), 3 + (n % 2), 5 + (n % 2)
                        S.op("pe", lambda e: e.matmul(PS[b_s][:, 0:128], kT, qT, start=True, stop=True), reads=[Bqkg[gs]], writes=[PB[b_s]])
                        S.op("dve", lambda e: e.tensor_tensor(out=sTd[:, k2, :], in0=PS[b_s][:, 0:128], in1=tsl("decT", h * 128, (h + 1) * 128), op=ALU.mult),
                             reads=[PB[b_s], Btabs], writes=[BsTd[k2]])
                        S.op("pool", lambda e: e.tensor_tensor(out=qx[:, k2, :], in0=qT, in1=tsl("xiT", h * 128, (h + 1) * 128), op=ALU.mult),
                             reads=[Bqkg[gs], Btabs], writes=[Bqx[k2]])
                        S.op("pool", lambda e: e.tensor_scalar(out=kz[:, k2, :], in0=ktok, scalar1=tcol("zeta", h), scalar2=None, op0=ALU.mult),
                             reads=[Bktv[sl], Btabs], writes=[Bkz[k2]])
                        S.op("pe", lambda e: e.matmul(PS[b_y][:, 0:128], vtok, sTd[:, k2, :], start=True, stop=False),
                             reads=[Bktv[sl], BsTd[k2]], writes=[PB[b_y]], inc=False)
                        S.op("pe", lambda e: e.matmul(PS[b_y][:, 0:128], Rb[:, h, :], qx[:, k2, :], start=False, stop=True),
                             reads=[BRb[h], Bqx[k2]], writes=[PB[b_y]])
                        S.op("pe", lambda e: e.matmul(PS[b_kv][:, 0:128], kz[:, k2, :], vtok, start=True, stop=True),
                             reads=[Bkz[k2], Bktv[sl]], writes=[PB[b_kv]])
                        S.op("act", lambda e: e.copy(out=yb[:, h, ci * 128:(ci + 1) * 128], in_=PS[b_y][:, 0:128]), reads=[PB[b_y]], writes=[Byb[h]])
                        S.op("dve", lambda e: e.scalar_tensor_tensor(out=R[:, h, :], in0=R[:, h, :], scalar=GC[h], in1=PS[b_kv][:, 0:128],
                                                                     op0=ALU.mult, op1=ALU.add), reads=[BR[h], PB[b_kv]], writes=[BR[h]])
                        S.op("dve", lambda e: e.tensor_copy(out=Rb[:, h, :], in_=R[:, h, :]), reads=[BR[h]], writes=[BRb[h]])
                for h in range(6):
                    k2 = h % 2
                    S.op("pe", lambda e: e.matmul(PS[7], cmat, yb[:, h, :], start=True, stop=True), reads=[Btabb, Byb[h]], writes=[PB[7]])
                    _act(S, ysq[:, k2, :], PS[7], AF.Square, [PB[7]], [Bysq[k2]])
                    S.op("pe", lambda e: e.matmul(PS[0], C.ones, ysq[:, k2, :], start=True, stop=True), reads=[C.Bones, Bysq[k2]], writes=[PB[0]])
                    _act(S, rs2, PS[0], AF.Sqrt, [PB[0]], [Brs2], bias=EPS, scale=1.0 / 128)
                    S.op("dve", lambda e: e.reciprocal(out=rs2, in_=rs2), reads=[Brs2], writes=[Brs2])
                    _act(S, sgt, qkg[:, gs, 12 + h, :], AF.Silu, [Bqkg[gs]], [Bsgt])
                    S.op("dve", lambda e: e.tensor_tensor(out=t1, in0=PS[7], in1=rs2, op=ALU.mult), reads=[PB[7], Brs2], writes=[Bt1])
                    S.op("dve", lambda e: e.scalar_tensor_tensor(out=ycat[:, h, g * 512:(g + 1) * 512], in0=t1, scalar=tcol("retg", h), in1=sgt,
                                                                 op0=ALU.mult, op1=ALU.mult), reads=[Bt1, Btabs, Bsgt], writes=[By[h][g]])
            S.end_phase()

        with SB(nc, "Vh", [128, 2, 64, 128], BF16) as Vh_, SB(nc, "KT", [65, 2, 8192], BF16) as KT_, \
                SB(nc, "QT", [65, 2, TL], BF16) as QT_, SB(nc, "pT", [128, 4, 512], BF16) as pT_, \
                SB(nc, "om", [128, 2, TL], F32) as om_, SB(nc, "rz", [128, 512], F32) as rz_, \
                SB(nc, "asq", [128, 512], BF16) as asq_, SB(nc, "rs3", [128, 512], F32) as rs3_, \
                SB(nc, "sg2", [128, 8], F32) as sg2_:
            Vh, KT, QT, pT, om, rz, asq, rs3, sg2 = (t.ap() for t in (Vh_, KT_, QT_, pT_, om_, rz_, asq_, rs3_, sg2_))
            BVh = [S.buf("Vh0"), S.buf("Vh1")]
            BKT = [S.buf("KT0"), S.buf("KT1")]
            BQT = [S.buf("QT0"), S.buf("QT1")]
            BpT = [S.buf("pT%d" % i) for i in range(4)]
            Bom = [S.buf("om0"), S.buf("om1")]
            Brz, Basq, Brs3, Bsg2 = S.buf("rz"), S.buf("asq"), S.buf("rs3"), S.buf("sg2")
            for sl in range(2):
                S.op("dve", lambda e: e.memset(KT[64:65, sl, :], 1.0), writes=[BKT[sl]])
            S.op("dve", lambda e: e.tensor_scalar(out=sg2[:, 0:6], in0=tsl("subg", 0, 6), scalar1=tcol("lconst", 1), scalar2=None, op0=ALU.mult),
                 reads=[Btabs], writes=[Bsg2])
            hm = 0
            for h in range(6):
                vs = h % 2
                for j in range(4):
                    S.dma("sp", Vh[:, vs, j * 16:(j + 1) * 16, :], vall[j, :, :, h * 128:(h + 1) * 128].rearrange("t p d -> p t d"), BVh[vs], Bro)
                for m in range(2):
                    ks = hm % 2
                    hm += 1
                    for j in range(4):
                        S.dma("sp", KT[0:64, ks, j * 2048:(j + 1) * 2048], kall[j, h, m * 64:(m + 1) * 64, :], BKT[ks], Bro)
                    S.dma("sp", QT[0:64, ks, :], pfm[18 + h, m * 64:(m + 1) * 64, :], BQT[ks], Bpfm)
                    S.dma("sp", QT[64:65, ks, :], qaug_d[h:h + 1, :], BQT[ks], Bro)
                    for g in range(4):
                        qsl = QT[:, ks, g * 512:(g + 1) * 512]

                        def score(kt):
                            bk = 2 + (kt % 3)
                            j = kt // 16
                            diag = ((kt % 16) // 4 == g)
                            S.op("pe", lambda e: e.matmul(PS[bk], KT[:, ks, kt * 128:(kt + 1) * 128], qsl, start=True, stop=(not diag)),
                                 reads=[BKT[ks], BQT[ks]], writes=[PB[bk]], inc=(not diag))
                            if diag:
                                off = 384 - (kt % 4) * 128
                                S.op("pe", lambda e: e.matmul(PS[bk], dj[:, j * 128:(j + 1) * 128], wmask[:, off:off + 512], start=False, stop=True),
                                     reads=[Btabb], writes=[PB[bk]])
                        score(0)
                        score(1)
                        for kt in range(64):
                            bk = 2 + (kt % 3)
                            pb4 = kt % 4
                            bo = _TAB["bias"][0] + (h * 4 + g) * 64 + kt
                            S.op("act", lambda e: e.activation(out=pT[:, pb4, :], in_=PS[bk], func=AF.Exp, bias=tabs[:, bo:bo + 1], scale=0.125),
                                 reads=[PB[bk], Btabs], writes=[BpT[pb4]])
                            if kt + 2 < 64:
                                score(kt + 2)
                            S.op("pe", lambda e: e.matmul(PS[0], Vh[:, vs, kt, :], pT[:, pb4, :], start=(kt == 0), stop=(kt == 63)),
                                 reads=[BVh[vs], BpT[pb4]], writes=[PB[0]], inc=False)
                            S.op("pe", lambda e: e.matmul(PS[1], C.ones, pT[:, pb4, :], start=(kt == 0), stop=(kt == 63)),
                                 reads=[C.Bones, BpT[pb4]], writes=[PB[1]])
                        S.op("dve", lambda e: e.reciprocal(out=rz, in_=PS[1]), reads=[PB[1]], writes=[Brz])
                        S.op("dve", lambda e: e.tensor_tensor(out=om[:, m, g * 512:(g + 1) * 512], in0=PS[0], in1=rz, op=ALU.mult),
                             reads=[PB[0], Brz], writes=[Bom[m]])
                for g in range(4):
                    a = om[:, 0, g * 512:(g + 1) * 512]
                    S.op("dve", lambda e: e.scalar_tensor_tensor(out=a, in0=om[:, 1, g * 512:(g + 1) * 512], scalar=lam[:, 4:5], in1=a,
                                                                 op0=ALU.mult, op1=ALU.add), reads=[Bom[1], Blam, Bom[0]], writes=[Bom[0]])
                    _act(S, asq, a, AF.Square, [Bom[0]], [Basq])
                    S.op("pe", lambda e: e.matmul(PS[7], C.ones, asq, start=True, stop=True), reads=[C.Bones, Basq], writes=[PB[7]])
                    _act(S, rs3, PS[7], AF.Sqrt, [PB[7]], [Brs3], bias=EPS, scale=1.0 / 128)
                    S.op("dve", lambda e: e.reciprocal(out=rs3, in_=rs3), reads=[Brs3], writes=[Brs3])
                    S.op("dve", lambda e: e.scalar_tensor_tensor(out=ycat[:, 6 + h, g * 512:(g + 1) * 512], in0=a, scalar=sg2[:, h:h + 1], in1=rs3,
                                                                 op0=ALU.mult, op1=ALU.mult), reads=[Bom[0], Bsg2, Brs3], writes=[By[6 + h][g]])
            S.end_phase()

        with SB(nc, "wo", [128, 3, 16, 128], BF16) as wo_, SB(nc, "sto", [128, 2, TL], F32) as sto_:
            wo, sto = wo_.ap(), sto_.ap()
            if "dbg_ycat" in io:
                Bdbg = S.buf("dbg_ycat", persist=True)
                for c in range(16):
                    S.dma("sp", io["dbg_ycat"][c], ycat[:, c, :], Bdbg, By[c][0])
            Bwo = [S.buf("wo%d" % i) for i in range(3)]
            Bsto = [S.buf("sto0"), S.buf("sto1")]
            ev = 0
            for nb in range(16):
                sl = nb % 3
                st = nb % 2
                S.dma("pool", wo[:, sl], wout[nb], Bwo[sl], Bro)
                for tg in range(4):
                    bk = C.bank_rr % 8
                    C.bank_rr += 1
                    for kc in range(16):
                        S.op("pe", lambda e: e.matmul(PS[bk], wo[:, sl, kc, :], ycat[:, kc, tg * 512:(tg + 1) * 512], start=(kc == 0), stop=(kc == 15)),
                             reads=[Bwo[sl], By[kc][tg]], writes=[PB[bk]], inc=(kc == 15))
                    dst = sto[:, st, tg * 512:(tg + 1) * 512]
                    if ev % 2 == 0:
                        S.op("act", lambda e: e.copy(out=dst, in_=PS[bk]), reads=[PB[bk]], writes=[Bsto[st]])
                    else:
                        S.op("dve", lambda e: e.tensor_copy(out=dst, in_=PS[bk]), reads=[PB[bk]], writes=[Bsto[st]])
                    ev += 1
                S.dma("sp", mixT[nb], sto[:, st, :], BmixT, Bsto[st])
            S.end_phase()

    h2T = C.big
    Bh2 = [S.buf("h2_%d" % g, persist=True) for g in range(4)]
    with SB(nc, "gB", [128, 48], F32) as gB_:
        gB = gB_.ap()
        BgB = S.buf("gB", persist=True)
        o_gB = _TAB["gB"][0]
        S.dma("sp", gB, tabs_d[:, o_gB:o_gB + 48], BgB, Bro)
        with SB(nc, "xg", [128, 16, 512], F32) as xg_, SB(nc, "mg", [128, 16, 512], F32) as mg_, \
                SB(nc, "sq", [128, 2, 512], BF16) as sq_, SB(nc, "rstd", [128, 512], F32) as rstd_:
            xg, mg, sq, rstd = xg_.ap(), mg_.ap(), sq_.ap(), rstd_.ap()
            Bxg, Bmg, Brstd = S.buf("xg"), S.buf("mg"), S.buf("rstd")
            Bsq = [S.buf("sq0"), S.buf("sq1")]
            for g in range(4):
                S.dma("sp", mg, mixT[:, :, g * 512:(g + 1) * 512].rearrange("c p t -> p c t"), Bmg, BmixT)
                S.dma("sp", xg, xT[:, :, g * 512:(g + 1) * 512].rearrange("c p t -> p c t"), Bxg, Bx)
                _rms_group(C, mg, Bmg, rstd, Brstd, [sq[:, 0, :], sq[:, 1, :]], Bsq, 0, 16, 1.0 / D)
                for c in range(16):
                    S.op("dve", lambda e: e.scalar_tensor_tensor(out=mg[:, c, :], in0=mg[:, c, :], scalar=gB[:, c:c + 1], in1=rstd,
                                                                 op0=ALU.mult, op1=ALU.mult), reads=[Bmg, BgB, Brstd], writes=[Bmg])
                    S.op("pool", lambda e: e.tensor_tensor(out=xg[:, c, :], in0=xg[:, c, :], in1=mg[:, c, :], op=ALU.add),
                         reads=[Bxg, Bmg], writes=[Bxg])
                S.dma("sp", xo[:, :, g * 512:(g + 1) * 512].rearrange("c p t -> p c t"), xg, Bxo, Bxg)
                if "dbg_xmid" in io:
                    S.dma("sp", io["dbg_xmid"][:, :, g * 512:(g + 1) * 512].rearrange("c p t -> p c t"), xg, io["Bdbg_xmid"], Bxg)
                _rms_group(C, xg, Bxg, rstd, Brstd, [sq[:, 0, :], sq[:, 1, :]], Bsq, 0, 16, 1.0 / D)
                for c in range(16):
                    S.op("dve", lambda e: e.scalar_tensor_tensor(out=h2T[:, c, g * 512:(g + 1) * 512], in0=xg[:, c, :], scalar=gB[:, 16 + c:17 + c],
                                                                 in1=rstd, op0=ALU.mult, op1=ALU.mult), reads=[Bxg, BgB, Brstd], writes=[Bh2[g]])
            S.end_phase()
        with SB(nc, "actT", [128, NHC, 1024], BF16) as actT_, SB(nc, "wgu", [128, 2, 2, 16, 128], BF16) as wgu_, \
                SB(nc, "wdn", [128, 2, NHC, 128], BF16) as wdn_, SB(nc, "sgl", [128, 2, 512], F32) as sgl_, \
                SB(nc, "stf", [128, 2, 1024], F32) as stf_:
            actT, wgu, wdn, sgl, stf = actT_.ap(), wgu_.ap(), wdn_.ap(), sgl_.ap(), stf_.ap()
            Bwgu = [S.buf("wgu0"), S.buf("wgu1")]
            Bwdn = [S.buf("wdn0"), S.buf("wdn1")]
            Bsgl = [S.buf("sgl0"), S.buf("sgl1")]
            Bstf = [S.buf("stf0"), S.buf("stf1")]
            for hf in range(2):
                Bact = [[S.buf("act%d_%d" % (j, t2)) for t2 in range(2)] for j in range(NHC)]
                n = 0
                for j in range(NHC):
                    sl = j % 2
                    S.dma("pool", wgu[:, sl, 0], wg[j], Bwgu[sl], Bro)
                    S.dma("pool", wgu[:, sl, 1], wu[j], Bwgu[sl], Bro)
                    for t2 in range(2):
                        tg = hf * 2 + t2
                        bg, bu = (n % 4) * 2, (n % 4) * 2 + 1
                        k2 = n % 2
                        n += 1
                        for kc in range(16):
                            S.op("pe", lambda e: e.matmul(PS[bg], wgu[:, sl, 0, kc, :], h2T[:, kc, tg * 512:(tg + 1) * 512], start=(kc == 0), stop=(kc == 15)),
                                 reads=[Bwgu[sl], Bh2[tg]], writes=[PB[bg]], inc=(kc == 15))
                        for kc in range(16):
                            S.op("pe", lambda e: e.matmul(PS[bu], wgu[:, sl, 1, kc, :], h2T[:, kc, tg * 512:(tg + 1) * 512], start=(kc == 0), stop=(kc == 15)),
                                 reads=[Bwgu[sl], Bh2[tg]], writes=[PB[bu]], inc=(kc == 15))
                        _act(S, sgl[:, k2, :], PS[bg], AF.Silu, [PB[bg]], [Bsgl[k2]])
                        S.op("dve", lambda e: e.tensor_tensor(out=actT[:, j, t2 * 512:(t2 + 1) * 512], in0=PS[bu], in1=sgl[:, k2, :], op=ALU.mult),
                             reads=[PB[bu], Bsgl[k2]], writes=[Bact[j][t2]])
                ev = 0
                for nb in range(16):
                    sl = nb % 2
                    S.dma("pool", wdn[:, sl], wd[nb], Bwdn[sl], Bro)
                    for t2 in range(2):
                        bk = C.bank_rr % 8
                        C.bank_rr += 1
                        for kc in range(NHC):
                            S.op("pe", lambda e: e.matmul(PS[bk], wdn[:, sl, kc, :], actT[:, kc, t2 * 512:(t2 + 1) * 512], start=(kc == 0), stop=(kc == NHC - 1)),
                                 reads=[Bwdn[sl], Bact[kc][t2]], writes=[PB[bk]], inc=(kc == NHC - 1))
                        dst = stf[:, sl, t2 * 512:(t2 + 1) * 512]
                        if ev % 2 == 0:
                            S.op("act", lambda e: e.copy(out=dst, in_=PS[bk]), reads=[PB[bk]], writes=[Bstf[sl]])
                        else:
                            S.op("dve", lambda e: e.tensor_copy(out=dst, in_=PS[bk]), reads=[PB[bk]], writes=[Bstf[sl]])
                        ev += 1
                    S.dma("sp", fT[nb, :, hf * 1024:(hf + 1) * 1024], stf[:, sl, :], BfT, Bstf[sl])
            S.end_phase()
        with SB(nc, "xg", [128, 16, 512], F32) as xg_, SB(nc, "mg", [128, 16, 512], F32) as mg_, \
                SB(nc, "sq", [128, 2, 512], BF16) as sq_, SB(nc, "rstd", [128, 512], F32) as rstd_:
            xg, mg, sq, rstd = xg_.ap(), mg_.ap(), sq_.ap(), rstd_.ap()
            Bxg, Bmg, Brstd = S.buf("xg"), S.buf("mg"), S.buf("rstd")
            Bsq = [S.buf("sq0"), S.buf("sq1")]
            for g in range(4):
                S.dma("sp", mg, fT[:, :, g * 512:(g + 1) * 512].rearrange("c p t -> p c t"), Bmg, BfT)
                S.dma("sp", xg, xo[:, :, g * 512:(g + 1) * 512].rearrange("c p t -> p c t"), Bxg, Bxo)
                _rms_group(C, mg, Bmg, rstd, Brstd, [sq[:, 0, :], sq[:, 1, :]], Bsq, 0, 16, 1.0 / D)
                for c in range(16):
                    S.op("dve", lambda e: e.scalar_tensor_tensor(out=mg[:, c, :], in0=mg[:, c, :], scalar=gB[:, 32 + c:33 + c], in1=rstd,
                                                                 op0=ALU.mult, op1=ALU.mult), reads=[Bmg, BgB, Brstd], writes=[Bmg])
                    S.op("pool", lambda e: e.tensor_tensor(out=xg[:, c, :], in0=xg[:, c, :], in1=mg[:, c, :], op=ALU.add),
                         reads=[Bxg, Bmg], writes=[Bxg])
                S.dma("sp", xo[:, :, g * 512:(g + 1) * 512].rearrange("c p t -> p c t"), xg, Bxo, Bxg)
            S.end_phase()


def build_A():
    nc = bass.Bass("TRN2", target_bir_lowering=False)
    C = _setup(nc)
    S = C.S
    io = {}
    io["xT"] = nc.dram_tensor("xT", [16, 128, TL], F32, kind="ExternalInput").ap()
    io["win_fm"] = nc.dram_tensor("win_fm", [34, 128, 16, 128], F32, kind="ExternalInput").ap()
    io["win_tm"] = nc.dram_tensor("win_tm", [128, 16, 2816], F32, kind="ExternalInput").ap()
    io["tabA"] = nc.dram_tensor("tabA", [128, 112], F32, kind="ExternalInput").ap()
    io["pfm"] = nc.dram_tensor("pfm", [34, 128, TL], BF16, kind="ExternalOutput").ap()
    io["ptm"] = nc.dram_tensor("ptm", [16, 128, 2816], BF16, kind="ExternalOutput").ap()
    io["sloc"] = nc.dram_tensor("sloc", [128, 768], F32, kind="ExternalOutput").ap()
    io["BxT"] = S.buf("xT", persist=True, ro=True)
    io["Bpfm"], io["Bptm"], io["Bsloc"] = S.buf("pfm", persist=True), S.buf("ptm", persist=True), S.buf("sloc", persist=True)
    stage_A(C, io)
    return nc


def build_B(debug=False):
    nc = bass.Bass("TRN2", target_bir_lowering=False)
    C = _setup(nc)
    S = C.S
    io = {}
    if debug:
        io["dbg_ycat"] = nc.dram_tensor("dbg_ycat", [16, 128, TL], BF16, kind="ExternalOutput").ap()
        io["dbg_xmid"] = nc.dram_tensor("dbg_xmid", [16, 128, TL], F32, kind="ExternalOutput").ap()
        io["Bdbg_xmid"] = S.buf("dbg_xmid", persist=True)
    io["xT"] = nc.dram_tensor("xT", [16, 128, TL], F32, kind="ExternalInput").ap()
    io["pfm"] = nc.dram_tensor("pfm", [34, 128, TL], BF16, kind="ExternalInput").ap()
    io["ptm"] = nc.dram_tensor("ptm", [16, 128, 2816], BF16, kind="ExternalInput").ap()
    io["kall"] = nc.dram_tensor("kall", [4, 6, 128, TL], BF16, kind="ExternalInput").ap()
    io["vall"] = nc.dram_tensor("vall", [4, 16, 128, 768], BF16, kind="ExternalInput").ap()
    io["sall"] = nc.dram_tensor("sall", [4, 128, 768], F32, kind="ExternalInput").ap()
    io["wout"] = nc.dram_tensor("wout", [16, 128, 16, 128], F32, kind="ExternalInput").ap()
    io["wg"] = nc.dram_tensor("wg", [NHC, 128, 16, 128], F32, kind="ExternalInput").ap()
    io["wu"] = nc.dram_tensor("wu", [NHC, 128, 16, 128], F32, kind="ExternalInput").ap()
    io["wd"] = nc.dram_tensor("wd", [16, 128, NHC, 128], F32, kind="ExternalInput").ap()
    io["tabs"] = nc.dram_tensor("tabs", [128, NTAB], F32, kind="ExternalInput").ap()
    io["tabb"] = nc.dram_tensor("tabb", [128, NTABB], BF16, kind="ExternalInput").ap()
    io["qaug"] = nc.dram_tensor("qaug", [6, TL], BF16, kind="ExternalInput").ap()
    io["xo"] = nc.dram_tensor("xo", [16, 128, TL], F32, kind="ExternalOutput").ap()
    io["mixT"] = nc.dram_tensor("mixT", [16, 128, TL], F32).ap()
    io["fT"] = nc.dram_tensor("fT", [16, 128, TL], F32).ap()
    io["BxT"] = S.buf("xT", persist=True, ro=True)
    io["Bpfm"], io["Bptm"] = S.buf("pfm", persist=True, ro=True), S.buf("ptm", persist=True, ro=True)
    io["Bxo"] = S.buf("xo", persist=True)
    stage_B(C, io)
    return nc


_FM_COLS = [(0, 768), (768, 1536), (2304, 3072), (3072, 3840), (3840, 4608), (5376, 5888)]
_TM_COLS = [(768, 1536), (1536, 2304), (4608, 5376), (5888, 6400)]


def _blk(w, nk, nn):
    return np.ascontiguousarray(w.reshape(nk, 128, nn, 128).transpose(2, 1, 0, 3))


def _pp(v):
    return np.ascontiguousarray(v.reshape(-1, 128).T)


def _rep(v):
    return np.ascontiguousarray(np.broadcast_to(v.reshape(1, -1), (128, v.size)))


def _gammas():
    return [1.0 - 2.0 ** (-5.0 - h) for h in range(6)]


def _const_A():
    t = np.zeros((128, 16, 6), np.float64)
    c = np.arange(128)
    for h, gmm in enumerate(_gammas()):
        lg = math.log(gmm)
        for i in range(16):
            t[:, i, h] = 128 ** -0.5 * np.exp(lg * (127 - c) + lg * 128 * (15 - i))
    return t.reshape(128, 96).astype(np.float32)


def _const_B(r):
    out = {}
    slopes = [2.0 ** (-8.0 * (h + 1) / 6) for h in range(6)]
    ki = np.arange(128, dtype=np.float64)
    bias = np.zeros((128, 6, 4, 64), np.float64)
    for h in range(6):
        for g in range(4):
            qb = r * 2048 + g * 512
            for kt in range(64):
                j = kt // 16
                future = (j > r) or (j == r and (kt % 16) // 4 > g)
                if future:
                    bias[:, h, g, kt] = NEG
                else:
                    bias[:, h, g, kt] = slopes[h] * (kt * 128 + ki - qb)
    out["bias"] = bias.reshape(128, -1)
    qa = np.zeros((6, TL), np.float64)
    qrel = np.tile(np.arange(512, dtype=np.float64), 4)
    for h in range(6):
        qa[h] = -8.0 * slopes[h] * qrel
    out["qaug"] = qa.astype(ml_dtypes.bfloat16)
    x = np.arange(896) - 384
    wm = np.where(ki[:, None] > x[None, :], NEG, 0.0)
    djm = np.zeros((128, 4, 128), np.float64)
    djm[:, r, :] = np.eye(128)
    cm = np.eye(128) - 1.0 / 128
    out["tabb"] = np.concatenate([wm, djm.reshape(128, 512), cm], axis=1).astype(ml_dtypes.bfloat16)
    dec = np.zeros((128, 6, 128), np.float64)
    zeta = np.zeros((128, 6), np.float64)
    xi = np.zeros((128, 6, 128), np.float64)
    coef = np.zeros((128, 4, 6), np.float64)
    e = np.arange(128)[:, None]
    c = np.arange(128)[None, :]
    for h, gmm in enumerate(_gammas()):
        lg = math.log(gmm)
        dec[:, h, :] = np.where(c >= e, np.exp(lg * np.maximum(c - e, 0)), 0.0) * 128 ** -0.5
        zeta[:, h] = np.exp(lg * (127 - np.arange(128))) * 128 ** -0.5
        xi[:, h, :] = np.exp(lg * (np.arange(128) + 1.0))[None, :]
        for j in range(4):
            coef[:, j, h] = math.exp(lg * 2048 * (r - 1 - j)) if j < r else 0.0
    out["decT"], out["zeta"], out["xiT"], out["coef"] = dec.reshape(128, -1), zeta, xi.reshape(128, -1), coef.reshape(128, -1)
    s = np.arange(128)[:, None]
    t = np.arange(128)[None, :]
    out["tril"] = (t >= s).astype(np.float64)
    return out


_PROGS = {}


def _prog(name):
    if name not in _PROGS:
        _PROGS[name] = build_A() if name == "A" else build_B()
    return _PROGS[name]


def kernel(x, pre_mix_g, w_in, ret_gn_g, diff_lam_q1, diff_lam_k1, diff_lam_q2, diff_lam_k2,
           diff_subln_g, sgu_ln_g, sgu_ln_b, sgu_w, sgu_b, w_out, post_mix_g, pre_ffn_g,
           w_gate, w_up, w_down, post_ffn_g):
    f32 = np.float32
    x = np.asarray(x, f32)
    ncore = 8
    xs = []
    for c in range(ncore):
        b, r = c // 4, c % 4
        xs.append(np.ascontiguousarray(x[b, r * TL:(r + 1) * TL, :].T).reshape(16, 128, TL))
    cA = _const_A()
    cB = [_const_B(r) for r in range(4)]
    for l in range(DEPTH):
        li = 0.8 - 0.6 * math.exp(-0.3 * l)
        wl = np.asarray(w_in[l], f32)
        win_fm = _blk(np.concatenate([wl[:, a:b] for a, b in _FM_COLS], axis=1), 16, 34)
        wtm = np.concatenate([wl[:, a:b] for a, b in _TM_COLS], axis=1)
        win_tm = np.ascontiguousarray(wtm.reshape(16, 128, 2816).transpose(1, 0, 2))
        tabA = np.concatenate([_pp(np.asarray(pre_mix_g[l], f32)), cA], axis=1).astype(f32)
        in_maps = [{"xT": xs[c], "win_fm": win_fm, "win_tm": win_tm, "tabA": tabA} for c in range(ncore)]
        resA = run_bass_kernel_spmd(_prog("A"), in_maps, core_ids=list(range(ncore))).results
        del win_fm, win_tm, wtm
        wout = _blk(np.asarray(w_out[l], f32), 16, 16)
        wg = _blk(np.asarray(w_gate[l], f32), 16, NHC)
        wu = _blk(np.asarray(w_up[l], f32), 16, NHC)
        wd = _blk(np.asarray(w_down[l], f32), NHC, 16)
        gB = np.concatenate([_pp(np.asarray(post_mix_g[l], f32)), _pp(np.asarray(pre_ffn_g[l], f32)), _pp(np.asarray(post_ffn_g[l], f32))], axis=1)
        common = {
            "gB": gB,
            "retg": np.asarray(ret_gn_g[l], f32).T,
            "subg": np.asarray(diff_subln_g[l], f32).T,
            "lng": _rep(np.asarray(sgu_ln_g[l], f32)),
            "lnb": _rep(np.asarray(sgu_ln_b[l], f32)),
            "sgub": _rep(np.asarray(sgu_b[l], f32)),
            "sguw": np.ascontiguousarray(np.asarray(sgu_w[l], f32).transpose(2, 0, 1)).reshape(128, 512),
            "lamv": np.concatenate([_rep(np.asarray(v[l], f32)) for v in (diff_lam_q1, diff_lam_k1, diff_lam_q2, diff_lam_k2)], axis=1),
            "lconst": _rep(np.array([li, 1.0 - li], f32)),
        }
        in_maps = []
        for c in range(ncore):
            b, r = c // 4, c % 4
            grp = [resA[b * 4 + j] for j in range(4)]
            tabs = np.zeros((128, NTAB), f32)
            for nme, (o, w) in _TAB.items():
                src = common[nme] if nme in common else cB[r][nme]
                tabs[:, o:o + w] = np.asarray(src, np.float64).reshape(128, w)
            in_maps.append({
                "xT": xs[c], "pfm": resA[c]["pfm"], "ptm": resA[c]["ptm"],
                "kall": np.stack([g_["pfm"][24:30] for g_ in grp], axis=0),
                "vall": np.stack([g_["ptm"][:, :, 1536:2304] for g_ in grp], axis=0),
                "sall": np.stack([g_["sloc"] for g_ in grp], axis=0),
                "wout": wout, "wg": wg, "wu": wu, "wd": wd,
                "tabs": tabs, "tabb": cB[r]["tabb"], "qaug": cB[r]["qaug"],
            })
        resB = run_bass_kernel_spmd(_prog("B"), in_maps, core_ids=list(range(ncore))).results
        xs = [resB[c]["xo"] for c in range(ncore)]
    out = np.empty((2, 8192, D), f32)
    for c in range(ncore):
        b, r = c // 4, c % 4
        out[b, r * TL:(r + 1) * TL, :] = xs[c].reshape(D, TL).T
    return out
```
